# Optimizing a Trainium2 kernel written in Bass

```python
import math
import jax, jax.numpy as jnp
from jax import lax
import numpy as np

D_MODEL = 2048
BATCH = 4
SEQ = 4096
DEPTH = 4

HEAD_DIM = 128
MIX_WIDTH = D_MODEL
A_HEADS = 4
A_NOPE = 128
A_ROPE = 64
A_V = 128
Q_LORA_RANK = 448
KV_LORA_RANK = 512
B_HEADS = 6
B_KV_HEADS = 2
B_GROUP = B_HEADS // B_KV_HEADS
GRID_W = 64
C_HEADS = 6
C_BRANCHES = ((128, 1), (512, 4), (2048, 16))
D_FF = 4 * D_MODEL
ROPE_THETA = 10000.0
Q_BLOCK = 128
EPS = 1e-6
NEG_INF = -1e30
IN_SIZES = (Q_LORA_RANK, KV_LORA_RANK, A_ROPE,
            B_HEADS * HEAD_DIM, B_KV_HEADS * HEAD_DIM, B_KV_HEADS * HEAD_DIM,
            C_HEADS * HEAD_DIM, C_HEADS * HEAD_DIM, C_HEADS * HEAD_DIM)
IN_WIDTH = sum(IN_SIZES)
OUT_SIZES = (A_HEADS * A_V, B_HEADS * HEAD_DIM, C_HEADS * HEAD_DIM)

kernel_name = "hymba_style_mla_gqa2d_dilated_encoder"


def _rms(x, g=None):
    xf = x.astype(jnp.float32)
    y = xf * lax.rsqrt(jnp.mean(xf * xf, axis=-1, keepdims=True) + EPS)
    if g is not None:
        y = y * g.astype(jnp.float32)
    return y.astype(x.dtype)


def _rope_angles(pos, dim):
    inv = jnp.power(ROPE_THETA, -jnp.arange(0, dim, 2, dtype=jnp.float32) / dim)
    ang = pos.astype(jnp.float32)[:, None] * inv[None, :]
    return jnp.cos(ang), jnp.sin(ang)


def _apply_rope(x, cs):
    cos, sin = cs
    cos = cos[:, None, :]
    sin = sin[:, None, :]
    xf = x.astype(jnp.float32)
    half = x.shape[-1] // 2
    x1, x2 = xf[..., :half], xf[..., half:]
    return jnp.concatenate([x1 * cos - x2 * sin, x1 * sin + x2 * cos], axis=-1).astype(x.dtype)


def _split_cols(a, sizes):
    out = []
    start = 0
    for s in sizes:
        out.append(a[..., start:start + s])
        start += s
    return out


def _dense_attn_blocked(q, k, v, scale):
    B, S, Hkv, G, Dk = q.shape
    nb = S // Q_BLOCK
    qb = q.reshape(B, nb, Q_BLOCK, Hkv, G, Dk).transpose(1, 0, 2, 3, 4, 5)

    def one_block(qblk):
        s = jnp.einsum('bqhgd,bkhd->bhgqk', qblk, k).astype(jnp.float32) * scale
        p = jax.nn.softmax(s, axis=-1)
        return jnp.einsum('bhgqk,bkhd->bqhgd', p.astype(v.dtype), v)

    ob = lax.map(one_block, qb)
    return ob.transpose(1, 0, 2, 3, 4, 5).reshape(B, S, Hkv, G, v.shape[-1])


def _band_attn(q, k, v, half, scale):
    N, L, H, D = q.shape
    blk = half
    nb = -(-L // blk)
    pad = nb * blk - L
    qp = jnp.pad(q, ((0, 0), (0, pad), (0, 0), (0, 0)))
    kp = jnp.pad(k, ((0, 0), (blk, pad + blk), (0, 0), (0, 0))).reshape(N, nb + 2, blk, H, D)
    vp = jnp.pad(v, ((0, 0), (blk, pad + blk), (0, 0), (0, 0))).reshape(N, nb + 2, blk, H, D)
    kwin = jnp.concatenate([kp[:, :-2], kp[:, 1:-1], kp[:, 2:]], axis=2)
    vwin = jnp.concatenate([vp[:, :-2], vp[:, 1:-1], vp[:, 2:]], axis=2)
    qb = qp.reshape(N, nb, blk, H, D)
    s = jnp.einsum('nbqhd,nbkhd->nbhqk', qb, kwin).astype(jnp.float32) * scale
    qpos = jnp.arange(nb)[:, None] * blk + jnp.arange(blk)[None, :]
    kpos = (jnp.arange(nb)[:, None] - 1) * blk + jnp.arange(3 * blk)[None, :]
    rel = kpos[:, None, :] - qpos[:, :, None]
    mask = (jnp.abs(rel) <= half) & (kpos[:, None, :] >= 0) & (kpos[:, None, :] < L)
    s = jnp.where(mask[None, :, None, :, :], s, NEG_INF)
    m = jnp.max(s, axis=-1, keepdims=True)
    e = jnp.exp(s - m)
    den = jnp.sum(e, axis=-1, keepdims=True)
    o = jnp.einsum('nbhqk,nbkhd->nbqhd', (e / den).astype(v.dtype), vwin)
    lse = (m + jnp.log(den))[..., 0]
    o = o.reshape(N, nb * blk, H, D)[:, :L]
    lse = lse.transpose(0, 1, 3, 2).reshape(N, nb * blk, H)[:, :L]
    return o, lse


def _dilated_mixture(q, k, v, scale):
    B, S, H, D = q.shape
    outs = []
    lses = []
    for window, dil in C_BRANCHES:
        half = window // (2 * dil)
        L = S // dil

        def to_sub(t):
            return t.reshape(B, L, dil, H, D).transpose(0, 2, 1, 3, 4).reshape(B * dil, L, H, D)

        o, lse = _band_attn(to_sub(q), to_sub(k), to_sub(v), half, scale)
        outs.append(o.reshape(B, dil, L, H, D).transpose(0, 2, 1, 3, 4).reshape(B, S, H, D))
        lses.append(lse.reshape(B, dil, L, H).transpose(0, 2, 1, 3).reshape(B, S, H))
    w = jax.nn.softmax(jnp.stack(lses, axis=0), axis=0)
    o = jnp.stack(outs, axis=0).astype(jnp.float32)
    return jnp.sum(w[..., None] * o, axis=0).astype(q.dtype)


def setup_inputs(seed: int = 0) -> dict:
    key = jax.random.key(seed)
    ks = jax.random.split(key, 16)
    f32 = jnp.float32

    def nrm(k, shape, scale):
        return jax.random.normal(k, shape, f32) * scale

    def gain(k, shape):
        return 1.0 + 0.02 * jax.random.normal(k, shape, f32)

    return {
        "x": jax.random.normal(ks[0], (BATCH, SEQ, D_MODEL), f32),
        "ln1_g": gain(ks[1], (DEPTH, D_MODEL)),
        "w_in": nrm(ks[2], (DEPTH, D_MODEL, IN_WIDTH), D_MODEL ** -0.5),
        "g_q_a": gain(ks[3], (DEPTH, Q_LORA_RANK)),
        "w_uq": nrm(ks[4], (DEPTH, Q_LORA_RANK, A_HEADS * (A_NOPE + A_ROPE)), Q_LORA_RANK ** -0.5),
        "g_kv_a": gain(ks[5], (DEPTH, KV_LORA_RANK)),
        "w_ukv": nrm(ks[6], (DEPTH, KV_LORA_RANK, A_HEADS * (A_NOPE + A_V)), KV_LORA_RANK ** -0.5),
        "g_qn_b": gain(ks[7], (DEPTH, HEAD_DIM)),
        "g_kn_b": gain(ks[8], (DEPTH, HEAD_DIM)),
        "g_out": gain(ks[9], (DEPTH, MIX_WIDTH)),
        "w_out": nrm(ks[10], (DEPTH, MIX_WIDTH, D_MODEL), MIX_WIDTH ** -0.5),
        "ln2_g": gain(ks[11], (DEPTH, D_MODEL)),
        "w_ff1": nrm(ks[12], (DEPTH, D_MODEL, D_FF), D_MODEL ** -0.5),
        "w_ff2": nrm(ks[13], (DEPTH, D_FF, D_MODEL), D_FF ** -0.5),
        "ln_f_g": gain(ks[14], (D_MODEL,)),
    }


def reference(x, ln1_g, w_in, g_q_a, w_uq, g_kv_a, w_ukv, g_qn_b, g_kn_b, g_out, w_out,
              ln2_g, w_ff1, w_ff2, ln_f_g):
    B, S, _ = x.shape
    ROWS = S // GRID_W
    pos = jnp.arange(S, dtype=jnp.float32)
    row = jnp.repeat(jnp.arange(ROWS, dtype=jnp.float32), GRID_W)
    col = jnp.tile(jnp.arange(GRID_W, dtype=jnp.float32), ROWS)
    cs_a = _rope_angles(pos, A_ROPE)
    cs_c = _rope_angles(pos, HEAD_DIM)
    cs_row = _rope_angles(row, HEAD_DIM // 2)
    cs_col = _rope_angles(col, HEAD_DIM // 2)
    scale_a = 1.0 / math.sqrt(A_NOPE + A_ROPE)
    scale_h = 1.0 / math.sqrt(HEAD_DIM)

    def axial(t):
        hd = HEAD_DIM // 2
        return jnp.concatenate([_apply_rope(t[..., :hd], cs_row), _apply_rope(t[..., hd:], cs_col)], axis=-1)

    for l in range(DEPTH):
        h = _rms(x, ln1_g[l])
        proj = h @ w_in[l]
        a_cq, a_ckv, a_kr, b_q, b_k, b_v, c_q, c_k, c_v = _split_cols(proj, IN_SIZES)

        qa = (_rms(a_cq, g_q_a[l]) @ w_uq[l]).reshape(B, S, A_HEADS, A_NOPE + A_ROPE)
        qa = jnp.concatenate([qa[..., :A_NOPE], _apply_rope(qa[..., A_NOPE:], cs_a)], axis=-1)
        kva = (_rms(a_ckv, g_kv_a[l]) @ w_ukv[l]).reshape(B, S, A_HEADS, A_NOPE + A_V)
        k_pe = _apply_rope(a_kr[:, :, None, :], cs_a)
        ka = jnp.concatenate([kva[..., :A_NOPE], jnp.broadcast_to(k_pe, (B, S, A_HEADS, A_ROPE))], axis=-1)
        va = kva[..., A_NOPE:]
        o_a = _dense_attn_blocked(qa[:, :, :, None, :], ka, va, scale_a).reshape(B, S, OUT_SIZES[0])

        qb = axial(_rms(b_q.reshape(B, S, B_HEADS, HEAD_DIM), g_qn_b[l]))
        kb = axial(_rms(b_k.reshape(B, S, B_KV_HEADS, HEAD_DIM), g_kn_b[l]))
        vb = b_v.reshape(B, S, B_KV_HEADS, HEAD_DIM)
        qb = qb.reshape(B, S, B_KV_HEADS, B_GROUP, HEAD_DIM)
        o_b = _dense_attn_blocked(qb, kb, vb, scale_h).reshape(B, S, OUT_SIZES[1])

        qc = _apply_rope(c_q.reshape(B, S, C_HEADS, HEAD_DIM), cs_c)
        kc = _apply_rope(c_k.reshape(B, S, C_HEADS, HEAD_DIM), cs_c)
        vc = c_v.reshape(B, S, C_HEADS, HEAD_DIM)
        o_c = _dilated_mixture(qc, kc, vc, scale_h).reshape(B, S, OUT_SIZES[2])

        mixed = jnp.concatenate([_rms(o_a), _rms(o_b), _rms(o_c)], axis=-1) * g_out[l]
        x = x + mixed @ w_out[l]

        u = jnp.square(jax.nn.relu(_rms(x, ln2_g[l]) @ w_ff1[l]))
        x = x + u @ w_ff2[l]

    return _rms(x, ln_f_g)
```

```python
import contextlib
import math
import numpy as np
import ml_dtypes
import concourse.bass as bass
import concourse.mybir as mybir
from concourse.bass_utils import run_bass_kernel_spmd

F32 = mybir.dt.float32
BF16 = mybir.dt.bfloat16
AF = mybir.ActivationFunctionType
ALU = mybir.AluOpType

D = 2048
SEQ = 4096
DEPTH = 4
TOK = 2048
NT = 4
TW = 512
EPS = 1e-6
NG = 74
G1, GQA, GKVA, GQN, GKN, GOUT, G2, GF = 0, 16, 20, 24, 25, 26, 42, 58
QT_ROWS = 2304
KT_ROWS = 1664
V_COLS = 1536
SCALE_A = 1.0 / math.sqrt(192.0)
SCALE_H = 1.0 / math.sqrt(128.0)
DEBUG = False


class _Eng:
    def __init__(self, name, eng, sem):
        self.name = name
        self.eng = eng
        self.sem = sem
        self.count = 0
        self.pending = False
        self.seen = {}
        self.nwaits = 0
        self.ninstr = 0


class Sched:
    def __init__(self, nc, es, n_dma_sems=8):
        self.nc = nc
        self.E = {}
        for name, eng in [("pe", nc.tensor), ("act", nc.scalar), ("dve", nc.vector),
                          ("pool", nc.gpsimd), ("sp", nc.sync)]:
            self.E[name] = _Eng(name, eng, es.enter_context(nc.semaphore("tl_" + name)))
        self.semown = {id(e.sem): e for e in self.E.values()}
        self.dma_sems = {}
        for q in ["sp", "pool", "act"]:
            self.dma_sems[q] = [[es.enter_context(nc.semaphore("dq_%s%d" % (q, i))), 0]
                                for i in range(n_dma_sems)]
        self.dma_rr = {q: 0 for q in self.dma_sems}
        self.last_w = {}
        self.readers = {}

    def _wait(self, e, ev):
        sem, val = ev
        k = id(sem)
        if e.seen.get(k, 0) >= val:
            return
        own = self.semown.get(k)
        if own is not None and val > own.count:
            raise RuntimeError("dependency on un-inc'd instruction of %s (val %d > count %d) from %s"
                               % (own.name, val, own.count, e.name))
        e.eng.wait_ge(sem, val)
        e.seen[k] = val
        e.nwaits += 1

    def _deps(self, reads, writes):
        evs = []
        for k in reads:
            if k in self.last_w:
                evs.append(self.last_w[k])
        for k in writes:
            if k in self.last_w:
                evs.append(self.last_w[k])
            evs.extend(self.readers.get(k, ()))
        return evs

    def _record(self, ev, reads, writes):
        for k in reads:
            self.readers.setdefault(k, []).append(ev)
        for k in writes:
            self.last_w[k] = ev
            self.readers[k] = []

    def op(self, en, fn, reads=(), writes=(), inc=True, pe_acc=False):
        e = self.E[en]
        for ev in self._deps(reads, writes):
            if ev[0] is e.sem:
                continue
            self._wait(e, ev)
        if not pe_acc:
            for k in reads:
                ev = self.last_w.get(k)
                if ev is not None and ev[0] is e.sem:
                    self._wait(e, ev)
        ins = fn()
        e.ninstr += 1
        if inc:
            e.count += 1
            ins.then_inc(e.sem, 1)
            e.pending = False
            ev = (e.sem, e.count)
        else:
            e.pending = True
            ev = (e.sem, e.count + 1)
        self._record(ev, reads, writes)
        return ins

    def dma(self, q, fn, reads=(), writes=()):
        e = self.E[q]
        for ev in self._deps(reads, writes):
            self._wait(e, ev)
        pool = self.dma_sems[q]
        i = self.dma_rr[q]
        self.dma_rr[q] = (i + 1) % len(pool)
        slot = pool[i]
        if slot[1] > 0:
            self._wait(e, (slot[0], slot[1]))
        ins = fn()
        e.ninstr += 1
        slot[1] += 16
        ins.then_inc(slot[0], 16)
        ev = (slot[0], slot[1])
        self._record(ev, reads, writes)
        return ins

    def finish(self, en="sp"):
        e = self.E[en]
        for q, pool in self.dma_sems.items():
            for sem, val in pool:
                if val > 0:
                    self._wait(e, (sem, val))
        for o in self.E.values():
            if o is e:
                continue
            if o.pending:
                raise RuntimeError("engine %s ends with un-inc'd instruction" % o.name)
            if o.count > 0:
                self._wait(e, (o.sem, o.count))

    def barrier(self):
        evs = []
        for q, pool in self.dma_sems.items():
            for sem, val in pool:
                if val > 0:
                    evs.append((sem, val))
        for o in self.E.values():
            if o.pending:
                raise RuntimeError("barrier with un-inc'd instruction on %s" % o.name)
            if o.count > 0:
                evs.append((o.sem, o.count))
        for e in self.E.values():
            for ev in evs:
                if ev[0] is e.sem:
                    continue
                self._wait(e, ev)

    def stats(self):
        return {k: (e.ninstr, e.nwaits) for k, e in self.E.items()}


class Rot:
    def __init__(self, items):
        self.items = items
        self.i = 0

    def next(self):
        it = self.items[self.i % len(self.items)]
        self.i += 1
        return it


class K:
    pass


def _sb(k, name, shape, dt):
    return k.es.enter_context(k.nc.sbuf_tensor(name, shape, dt))


def _mk_rot(k, es, name, n, shape, dt):
    items = []
    for i in range(n):
        t = es.enter_context(k.nc.sbuf_tensor("%s%d" % (name, i), shape, dt))
        items.append((t, (name, i)))
    return Rot(items)


def setup_common(k):
    nc, es, S = k.nc, k.es, k.S
    k.ones = es.enter_context(nc.sbuf_tensor("s_ones", [128, 128], BF16))
    k.ident = es.enter_context(nc.sbuf_tensor("s_ident", [128, 128], BF16))
    k.perm = es.enter_context(nc.sbuf_tensor("s_perm", [128, 2, 128], F32))
    k.gains = es.enter_context(nc.sbuf_tensor("s_gains", [128, NG], F32))
    S.op("dve", lambda: nc.vector.memset(k.ones[:], 1.0), writes=[("ones",)])
    S.dma("sp", lambda: nc.sync.dma_start(out=k.perm[:], in_=k.d_perm[0:2].rearrange("a p m -> p a m")),
          writes=[("perm",)])
    S.dma("pool", lambda: nc.gpsimd.dma_start(out=k.ident[:], in_=k.d_perm[2]), writes=[("ident",)])
    S.dma("sp", lambda: nc.sync.dma_start(out=k.gains[:], in_=k.d_gains), writes=[("gains",)])


def emit_rstd(k, ssq_ps, ssq_key, n, out_ap, out_key, tmp):
    nc, S = k.nc, k.S
    t, tk = tmp
    S.op("act", lambda: nc.scalar.activation(out=t[:], in_=ssq_ps, func=AF.Ln, bias=k.epsb[:, 0:1], scale=1.0 / n),
         reads=[ssq_key, ("epsb",)], writes=[tk])
    S.op("act", lambda: nc.scalar.activation(out=out_ap, in_=t[:], func=AF.Exp, scale=-0.5),
         reads=[tk], writes=[out_key])


def emit_L1(k, d_xT, d_win, d_wuq, d_wukv, d_tabs, d_QT, d_KT, d_V, lay=""):
    nc, S = k.nc, k.S
    S.barrier()
    lay = lay + "s1_"
    es = contextlib.ExitStack()
    with es:
        def sb(name, shape, dt):
            return es.enter_context(nc.sbuf_tensor(lay + name, shape, dt))
        k.ps = [es.enter_context(nc.psum_tensor(lay + "ps%d" % i, [128, 512], F32)) for i in range(7)]
        k.psb = es.enter_context(nc.psum_tensor(lay + "psb", [128, 1024], BF16))
        hT = sb("hT", [128, 16, TOK], BF16)
        rstd1 = sb("rstd1", [128, TOK], F32)
        lat = sb("lat", [128, 4, TOK], BF16)
        rstdl = sb("rstdl", [128, TOK], F32)
        tab = sb("tab", [128, 2, TOK], F32)
        NWS = 4
        wslot = [sb("wslot%d" % i, [128, 16, 128], BF16) for i in range(NWS)]
        xin = _mk_rot(k, es, lay + "xin", 2, [128, 2, TW], F32)
        sqb = _mk_rot(k, es, lay + "sqb", 2, [128, TW], BF16)
        raw = _mk_rot(k, es, lay + "raw", 2, [128, TW], F32)
        t1 = _mk_rot(k, es, lay + "t1_", 2, [128, TW], F32)
        t2 = _mk_rot(k, es, lay + "t2_", 2, [128, TW], F32)
        ob = _mk_rot(k, es, lay + "ob", 3, [128, TW], BF16)
        vtm = _mk_rot(k, es, lay + "vtm", 2, [128, 4, 128], BF16)
        tmpf = _mk_rot(k, es, lay + "tmpf", 2, [128, TW], F32)
        acc = Rot([(k.ps[i], ("ps", i)) for i in range(3)])
        ssqb = (k.ps[3], ("ps", 3))
        aux = Rot([(k.ps[i], ("ps", i)) for i in (4, 5, 6)])
        psbr = Rot([(0, ("psb", 0)), (1, ("psb", 1))])
        g = k.gains
        ws_i = [0]

        def tsl(t):
            return slice(t * TW, (t + 1) * TW)

        for t in range(NT):
            for c2 in range(8):
                xb, xk = xin.next()
                S.dma("sp", lambda xb=xb, c2=c2, t=t: nc.sync.dma_start(
                    out=xb[:], in_=d_xT[c2 * 256:(c2 + 1) * 256, tsl(t)].rearrange("(a p) n -> p a n", p=128)),
                    writes=[xk])
                for a in range(2):
                    c = c2 * 2 + a
                    sq, sqk = sqb.next()
                    S.op("act", lambda xb=xb, a=a, sq=sq: nc.scalar.activation(out=sq[:], in_=xb[:, a, :], func=AF.Square),
                         reads=[xk], writes=[sqk])
                    S.op("pe", lambda sq=sq, c=c: nc.tensor.matmul(ssqb[0][:], lhsT=k.ones[:], rhs=sq[:], start=(c == 0), stop=(c == 15)),
                         reads=[sqk, ("ones",)], writes=[ssqb[1]], inc=True, pe_acc=True)
                    S.op("dve", lambda xb=xb, a=a, c=c, t=t: nc.vector.tensor_scalar(
                        out=hT[:, c, tsl(t)], in0=xb[:, a, :], scalar1=g[:, G1 + c:G1 + c + 1], scalar2=None, op0=ALU.mult),
                        reads=[xk, ("gains",)], writes=[("hT", c, t)])
            emit_rstd(k, ssqb[0][:], ssqb[1], float(D), rstd1[:, tsl(t)], ("rstd1", t), tmpf.next())

        def load_w(src_ap, nslots):
            i = ws_i[0] % NWS
            ws_i[0] += 1
            S.dma("pool", lambda: nc.gpsimd.dma_start(
                out=wslot[i][:, 0:nslots, :].rearrange("p a b -> p (a b)"), in_=src_ap),
                writes=[("wslot", i)])
            return wslot[i], ("wslot", i)

        def load_tab(mi):
            for j in range(2):
                S.dma("sp", lambda j=j: nc.sync.dma_start(out=tab[:, j, :], in_=d_tabs[mi, j]), writes=[("tab", j)])

        def proj_mm(wt, wk, slot0, nk, rhs_fn, rhs_keys_fn, t):
            a, ak = acc.next()
            for kc in range(nk):
                S.op("pe", lambda kc=kc: nc.tensor.matmul(a[:], lhsT=wt[:, slot0 + kc, :], rhs=rhs_fn(kc, t),
                                                          start=(kc == 0), stop=(kc == nk - 1)),
                     reads=[wk] + rhs_keys_fn(kc, t), writes=[ak], inc=(kc == nk - 1), pe_acc=True)
            return a, ak

        def h_rhs(kc, t):
            return hT[:, kc, tsl(t)]

        def h_keys(kc, t):
            return [("hT", kc, t)]

        def lat_rhs(kc, t):
            return lat[:, kc, tsl(t)]

        def lat_keys(kc, t):
            return [("lat", kc, t)]

        def rope(rw, rwk, t, pidx):
            ax, axk = aux.next()
            S.op("pe", lambda: nc.tensor.matmul(ax[:], lhsT=k.perm[:, pidx, :], rhs=rw[:], start=True, stop=True),
                 reads=[rwk, ("perm",)], writes=[axk])
            a1, a1k = t1.next()
            a2, a2k = t2.next()
            S.op("dve", lambda: nc.vector.tensor_tensor(out=a1[:], in0=rw[:], in1=tab[:, 0, tsl(t)], op=ALU.mult),
                 reads=[rwk, ("tab", 0)], writes=[a1k])
            S.op("dve", lambda: nc.vector.tensor_tensor(out=a2[:], in0=ax[:], in1=tab[:, 1, tsl(t)], op=ALU.mult),
                 reads=[axk, ("tab", 1)], writes=[a2k])
            o, okk = ob.next()
            S.op("pool", lambda: nc.gpsimd.tensor_tensor(out=o[:], in0=a1[:], in1=a2[:], op=ALU.add),
                 reads=[a1k, a2k], writes=[okk])
            return o, okk

        def store_rows(o, okk, dram, row0, t, nrows=128):
            S.dma("sp", lambda: nc.sync.dma_start(out=dram[row0:row0 + nrows, tsl(t)], in_=o[0:nrows, :]),
                  reads=[okk], writes=[(id(dram), row0, t)])

        def store_v(o, okk, col0, t):
            hb, hbk = psbr.next()
            for j in range(4):
                S.op("pe", lambda j=j: nc.tensor.transpose(out=k.psb[:, hb * 512 + j * 128: hb * 512 + (j + 1) * 128],
                                                           in_=o[:, j * 128:(j + 1) * 128], identity=k.ident[:]),
                     reads=[okk, ("ident",)], writes=[hbk], inc=(j == 3), pe_acc=True)
            v, vk = vtm.next()
            S.op("act", lambda: nc.scalar.copy(out=v[:].rearrange("p a b -> p (a b)"), in_=k.psb[:, hb * 512:(hb + 1) * 512]),
                 reads=[hbk], writes=[vk])
            S.dma("sp", lambda: nc.sync.dma_start(
                out=d_V[col0 // 128, :, t * 4:(t + 1) * 4, :], in_=v[:]),
                reads=[vk], writes=[(id(d_V), col0, t)])

        def evac_scaled(a, ak, rs_ap, rs_key, out_t, out_k):
            S.op("dve", lambda: nc.vector.tensor_tensor(out=out_t, in0=a[:], in1=rs_ap, op=ALU.mult),
                 reads=[ak, rs_key], writes=[out_k])

        load_tab(0)

        def latent_phase(units, gcol, n):
            wts = [load_w(d_win[u], 16) for u in units]
            for t in range(NT):
                for j, (wt, wk) in enumerate(wts):
                    a, ak = proj_mm(wt, wk, 0, 16, h_rhs, h_keys, t)
                    rw, rwk = raw.next()
                    evac_scaled(a, ak, rstd1[:, tsl(t)], ("rstd1", t), rw[:], rwk)
                    sq, sqk = sqb.next()
                    S.op("act", lambda rw=rw, sq=sq: nc.scalar.activation(out=sq[:], in_=rw[:], func=AF.Square),
                         reads=[rwk], writes=[sqk])
                    S.op("pe", lambda sq=sq, j=j: nc.tensor.matmul(ssqb[0][:], lhsT=k.ones[:], rhs=sq[:], start=(j == 0), stop=(j == 3)),
                         reads=[sqk, ("ones",)], writes=[ssqb[1]], inc=True, pe_acc=True)
                    S.op("pool", lambda rw=rw, j=j, t=t: nc.gpsimd.tensor_scalar(
                        out=lat[:, j, tsl(t)], in0=rw[:], scalar1=g[:, gcol + j:gcol + j + 1], scalar2=None, op0=ALU.mult),
                        reads=[rwk, ("gains",)], writes=[("lat", j, t)])
                emit_rstd(k, ssqb[0][:], ssqb[1], float(n), rstdl[:, tsl(t)], ("rstdl", t), tmpf.next())

        latent_phase([0, 1, 2, 3], GQA, 448)
        wt, wk = load_w(d_wuq[:, 0:2048], 16)
        for h in range(4):
            for t in range(NT):
                a, ak = proj_mm(wt, wk, h * 4, 4, lat_rhs, lat_keys, t)
                o, okk = ob.next()
                evac_scaled(a, ak, rstdl[:, tsl(t)], ("rstdl", t), o[:], okk)
                store_rows(o, okk, d_QT, h * 128, t)
        wt, wk = load_w(d_wuq[:, 2048:3072], 8)
        for j in range(2):
            for t in range(NT):
                a, ak = proj_mm(wt, wk, j * 4, 4, lat_rhs, lat_keys, t)
                rw, rwk = raw.next()
                evac_scaled(a, ak, rstdl[:, tsl(t)], ("rstdl", t), rw[:], rwk)
                o, okk = rope(rw, rwk, t, 0)
                store_rows(o, okk, d_QT, 512 + j * 128, t)
        latent_phase([4, 5, 6, 7], GKVA, 512)
        wt, wk = load_w(d_wukv[0], 16)
        for h in range(4):
            for t in range(NT):
                a, ak = proj_mm(wt, wk, h * 4, 4, lat_rhs, lat_keys, t)
                o, okk = ob.next()
                evac_scaled(a, ak, rstdl[:, tsl(t)], ("rstdl", t), o[:], okk)
                store_rows(o, okk, d_KT, h * 128, t)
        wt, wk = load_w(d_wukv[1], 16)
        for h in range(4):
            for t in range(NT):
                a, ak = proj_mm(wt, wk, h * 4, 4, lat_rhs, lat_keys, t)
                o, okk = ob.next()
                evac_scaled(a, ak, rstdl[:, tsl(t)], ("rstdl", t), o[:], okk)
                store_v(o, okk, h * 128, t)
        wt, wk = load_w(d_win[8], 16)
        for t in range(NT):
            a, ak = proj_mm(wt, wk, 0, 16, h_rhs, h_keys, t)
            rw, rwk = raw.next()
            evac_scaled(a, ak, rstd1[:, tsl(t)], ("rstd1", t), rw[:], rwk)
            o, okk = rope(rw, rwk, t, 0)
            store_rows(o, okk, d_KT, 512, t)

        def v_block(u, col0):
            wt, wk = load_w(d_win[u], 16)
            for t in range(NT):
                a, ak = proj_mm(wt, wk, 0, 16, h_rhs, h_keys, t)
                o, okk = ob.next()
                evac_scaled(a, ak, rstd1[:, tsl(t)], ("rstd1", t), o[:], okk)
                store_v(o, okk, col0, t)

        def rope_block(u, dram, row0, pidx, gcol=None):
            wt, wk = load_w(d_win[u], 16)
            for t in range(NT):
                a, ak = proj_mm(wt, wk, 0, 16, h_rhs, h_keys, t)
                rw, rwk = raw.next()
                evac_scaled(a, ak, rstd1[:, tsl(t)], ("rstd1", t), rw[:], rwk)
                if gcol is not None:
                    sq, sqk = sqb.next()
                    S.op("act", lambda: nc.scalar.activation(out=sq[:], in_=rw[:], func=AF.Square),
                         reads=[rwk], writes=[sqk])
                    ax, axk = aux.next()
                    S.op("pe", lambda: nc.tensor.matmul(ax[:], lhsT=k.ones[:], rhs=sq[:], start=True, stop=True),
                         reads=[sqk, ("ones",)], writes=[axk])
                    rs, rsk = tmpf.next()
                    emit_rstd(k, ax[:], axk, 128.0, rs[:], rsk, tmpf.next())
                    rw2, rw2k = raw.next()
                    S.op("dve", lambda: nc.vector.scalar_tensor_tensor(
                        out=rw2[:], in0=rw[:], scalar=g[:, gcol:gcol + 1], in1=rs[:], op0=ALU.mult, op1=ALU.mult),
                        reads=[rwk, rsk, ("gains",)], writes=[rw2k])
                    rw, rwk = rw2, rw2k
                o, okk = rope(rw, rwk, t, pidx)
                store_rows(o, okk, dram, row0, t)

        for j in range(2):
            v_block(17 + j, 512 + j * 128)
        load_tab(1)
        for h in range(6):
            rope_block(9 + h, d_QT, 768 + h * 128, 0, GQN)
        for h in range(2):
            rope_block(15 + h, d_KT, 640 + h * 128, 0, GKN)
        for h in range(6):
            v_block(31 + h, 768 + h * 128)
        load_tab(2)
        for h in range(6):
            rope_block(19 + h, d_QT, 1536 + h * 128, 1)
        for h in range(6):
            rope_block(25 + h, d_KT, 896 + h * 128, 1)


def _units_kmajor(w, ncolblk):
    K_ = w.shape[0]
    kc = K_ // 128
    u = w.reshape(kc, 128, ncolblk, 128).transpose(2, 1, 0, 3)
    return np.ascontiguousarray(u).reshape(ncolblk, 128, kc * 128)


def prep_layer_weights(inp, l):
    w_in = inp["w_in"][l]
    cols = []
    z64 = np.zeros((D, 64), np.float32)
    cols.append(w_in[:, 0:448]); cols.append(z64)
    cols.append(w_in[:, 448:960])
    cols.append(w_in[:, 960:1024]); cols.append(w_in[:, 960:1024])
    cols.append(w_in[:, 1024:4608])
    wp = np.concatenate(cols, axis=1)
    win_u = _units_kmajor(wp, 37)
    wq = np.zeros((512, 768), np.float32)
    wq[:448] = inp["w_uq"][l]
    cb = [wq[:, 192 * h:192 * h + 128] for h in range(4)]
    cb.append(np.concatenate([wq[:, 128:192], wq[:, 320:384]], axis=1))
    cb.append(np.concatenate([wq[:, 512:576], wq[:, 704:768]], axis=1))
    wq_p = np.concatenate(cb, axis=1)
    wuq_u = _units_kmajor(wq_p, 6)
    wuq_u = np.ascontiguousarray(wuq_u.transpose(1, 0, 2)).reshape(128, 6 * 512)
    wkv = inp["w_ukv"][l]
    kb = np.concatenate([wkv[:, 256 * h:256 * h + 128] for h in range(4)], axis=1)
    vb = np.concatenate([wkv[:, 256 * h + 128:256 * h + 256] for h in range(4)], axis=1)
    wukv_u = np.stack([
        np.ascontiguousarray(_units_kmajor(kb, 4).transpose(1, 0, 2)).reshape(128, 2048),
        np.ascontiguousarray(_units_kmajor(vb, 4).transpose(1, 0, 2)).reshape(128, 2048)])
    gains = np.zeros((128, NG), np.float32)
    gains[:, G1:G1 + 16] = inp["ln1_g"][l].reshape(16, 128).T
    gq = np.zeros(512, np.float32); gq[:448] = inp["g_q_a"][l]
    gains[:, GQA:GQA + 4] = gq.reshape(4, 128).T
    gains[:, GKVA:GKVA + 4] = inp["g_kv_a"][l].reshape(4, 128).T
    gains[:, GQN] = inp["g_qn_b"][l]
    gains[:, GKN] = inp["g_kn_b"][l]
    gains[:, GOUT:GOUT + 16] = inp["g_out"][l].reshape(16, 128).T
    gains[:, G2:G2 + 16] = inp["ln2_g"][l].reshape(16, 128).T
    gains[:, GF:GF + 16] = inp["ln_f_g"].reshape(16, 128).T
    return dict(win=win_u, wuq=wuq_u, wukv=wukv_u, gains=gains)


def rope_tables(half):
    pos = (np.arange(TOK) if half == 0 else (SEQ - 1 - np.arange(TOK))).astype(np.float32)
    tabs = np.zeros((3, 2, 128, TOK), np.float32)
    p = np.arange(128)
    inv64 = np.power(np.float32(10000.0), -np.arange(0, 64, 2, dtype=np.float32) / np.float32(64)).astype(np.float32)
    inv128 = np.power(np.float32(10000.0), -np.arange(0, 128, 2, dtype=np.float32) / np.float32(128)).astype(np.float32)
    sgn64 = np.where((p % 64) < 32, -1.0, 1.0).astype(np.float32)
    f64 = inv64[(p % 64) % 32]
    ang = (f64[:, None] * pos[None, :]).astype(np.float32)
    tabs[0, 0] = np.cos(ang); tabs[0, 1] = np.sin(ang) * sgn64[:, None]
    rowp = np.floor(pos / 64.0).astype(np.float32)
    colp = (pos - rowp * 64.0).astype(np.float32)
    pp = np.where(p[:, None] < 64, rowp[None, :], colp[None, :]).astype(np.float32)
    ang = (f64[:, None] * pp).astype(np.float32)
    tabs[1, 0] = np.cos(ang); tabs[1, 1] = np.sin(ang) * sgn64[:, None]
    f128 = inv128[p % 64]
    sgn128 = np.where(p < 64, -1.0, 1.0).astype(np.float32)
    ang = (f128[:, None] * pos[None, :]).astype(np.float32)
    tabs[2, 0] = np.cos(ang); tabs[2, 1] = np.sin(ang) * sgn128[:, None]
    return tabs


def perm_mats():
    pm = np.zeros((3, 128, 128), np.float32)
    m = np.arange(128)
    sw64 = np.where((m % 64) < 32, m + 32, m - 32)
    sw128 = np.where(m < 64, m + 64, m - 64)
    pm[0, sw64, m] = 1.0
    pm[1, sw128, m] = 1.0
    pm[2, m, m] = 1.0
    return pm


def build_L1():
    nc = bass.Bass("TRN2", target_bir_lowering=False)
    k = K()
    k.nc = nc
    d_xT = nc.dram_tensor("xT", [D, TOK], F32, kind="ExternalInput").ap()
    d_win = nc.dram_tensor("win", [37, 128, 2048], F32, kind="ExternalInput").ap()
    d_wuq = nc.dram_tensor("wuq", [128, 3072], F32, kind="ExternalInput").ap()
    d_wukv = nc.dram_tensor("wukv", [2, 128, 2048], F32, kind="ExternalInput").ap()
    k.d_gains = nc.dram_tensor("gains", [128, NG], F32, kind="ExternalInput").ap()
    d_tabs = nc.dram_tensor("tabs", [3, 2, 128, TOK], F32, kind="ExternalInput").ap()
    k.d_perm = nc.dram_tensor("perm", [3, 128, 128], F32, kind="ExternalInput").ap()
    d_QT = nc.dram_tensor("QT", [QT_ROWS, TOK], BF16, kind="ExternalOutput").ap()
    d_KT = nc.dram_tensor("KT", [KT_ROWS, TOK], BF16, kind="ExternalOutput").ap()
    d_V = nc.dram_tensor("V", [12, 128, 16, 128], BF16, kind="ExternalOutput").ap()
    es = contextlib.ExitStack()
    with es:
        k.es = es
        k.S = Sched(nc, es)
        k.epsb = es.enter_context(nc.sbuf_tensor("s_epsb", [128, 1], F32))
        k.S.op("dve", lambda: nc.vector.memset(k.epsb[:], EPS), writes=[("epsb",)])
        setup_common(k)
        emit_L1(k, d_xT, d_win, d_wuq, d_wukv, d_tabs, d_QT, d_KT, d_V)
        k.S.finish("sp")
        print("L1 stats", k.S.stats())
    return nc


def c_chunks(t):
    out = []
    for lc in range(16):
        delta = lc * 128 - t * 512
        if -1024 <= delta <= 1408:
            out.append((lc, (delta + 1024) // 128))
    for cp in range(16):
        s_ = 4 * t + cp
        if s_ >= 20:
            out.append((16 + cp, 20 + (s_ - 20)))
    return out


def emit_attn(k, half, d_QT, d_KTf, d_Vf, d_masks, mixedT, lay=""):
    nc, S = k.nc, k.S
    S.barrier()
    lay = lay + "sa_"
    es = contextlib.ExitStack()
    with es:
        def sb(name, shape, dt):
            return es.enter_context(nc.sbuf_tensor(lay + name, shape, dt))
        ps = [es.enter_context(nc.psum_tensor(lay + "aps%d" % i, [128, 512], F32)) for i in range(8)]
        Kb = _mk_rot(k, es, lay + "Kb", 2, [128, SEQ], BF16)
        Vb = _mk_rot(k, es, lay + "Vb", 2, [128, 32, 128], BF16)
        Kpe = sb("Kpe", [128, SEQ], BF16)
        Qb = _mk_rot(k, es, lay + "Qb", 2, [128, 6, TW], BF16)
        PT = _mk_rot(k, es, lay + "PT", 4, [128, TW], BF16)
        masks = sb("masks", [128, 28, TW], BF16)
        ogrp = sb("ogrp", [128, 6, TW], F32)
        rsb = _mk_rot(k, es, lay + "rsb", 2, [128, TW], F32)
        sqb = _mk_rot(k, es, lay + "asq", 2, [128, TW], BF16)
        tmpf = _mk_rot(k, es, lay + "atmp", 2, [128, TW], F32)
        rstdg = sb("rstdg", [128, TW], F32)
        stb = Rot([(ps[i], ("aps", i)) for i in range(3)])
        ob_ = Rot([(ps[3], ("aps", 3)), (ps[4], ("aps", 4))])
        sm_ = Rot([(ps[5], ("aps", 5)), (ps[6], ("aps", 6))])
        gss = (ps[7], ("aps", 7))
        g = k.gains

        for m4 in range(7):
            S.dma("sp", lambda m4=m4: nc.sync.dma_start(out=masks[:, m4 * 4:(m4 + 1) * 4, :],
                                                        in_=d_masks[m4 * 4:(m4 + 1) * 4].rearrange("a p n -> p a n")),
                  writes=[("masks", m4)])
        S.dma("sp", lambda: nc.sync.dma_start(out=Kpe[:].rearrange("p (r n) -> p r n", r=2),
                                              in_=d_KTf[:, 512:640, :].rearrange("r p n -> p r n")), writes=[("Kpe",)])

        def load_K(row0):
            kb, kk = Kb.next()
            S.dma("sp", lambda: nc.sync.dma_start(out=kb[:].rearrange("p (r n) -> p r n", r=2),
                                                  in_=d_KTf[:, row0:row0 + 128, :].rearrange("r p n -> p r n")), writes=[kk])
            return kb, kk

        def load_V(hidx):
            vb, vk = Vb.next()
            for r in range(2):
                S.dma("sp", lambda r=r: nc.sync.dma_start(out=vb[:, r * 16:(r + 1) * 16, :], in_=d_Vf[r, hidx]),
                      writes=[(vk, r)])
            return vb, [(vk, 0), (vk, 1)]

        def load_Q(row0, t):
            qb, qk = Qb.next()
            S.dma("sp", lambda: nc.sync.dma_start(out=qb[:], in_=d_QT[row0:row0 + 768, t * TW:(t + 1) * TW].rearrange("(a p) n -> p a n", p=128)),
                  writes=[qk])
            return qb, qk

        def head(qk_fn, qk_reads, chunks, vb, vkeys, scale, oi, first, last, use_mask):
            ot, otk = ob_.next()
            sm, smk = sm_.next()
            n = len(chunks)
            sts = [None] * n

            def emit_qk(i):
                st, stk = stb.next()
                qk_fn(i, st, stk)
                sts[i] = (st, stk)
            emit_qk(0)
            for i in range(n):
                if i + 1 < n:
                    emit_qk(i + 1)
                kc, mi = chunks[i]
                st, stk = sts[i]
                pt, ptk = PT.next()
                S.op("act", lambda: nc.scalar.activation(out=pt[:], in_=st[:], func=AF.Exp, scale=scale),
                     reads=[stk], writes=[ptk])
                if use_mask:
                    S.op("pool", lambda: nc.gpsimd.tensor_tensor(out=pt[:], in0=pt[:], in1=masks[:, mi, :], op=ALU.mult),
                         reads=[ptk, ("masks", mi // 4)], writes=[ptk])
                S.op("pe", lambda: nc.tensor.matmul(ot[:], lhsT=vb[:, kc, :], rhs=pt[:], start=(i == 0), stop=(i == n - 1)),
                     reads=[ptk, vkeys[kc // 16]], writes=[otk], inc=False, pe_acc=True)
                S.op("pe", lambda: nc.tensor.matmul(sm[:], lhsT=k.ones[:], rhs=pt[:], start=(i == 0), stop=(i == n - 1)),
                     reads=[ptk, ("ones",)], writes=[smk], inc=True, pe_acc=True)
            rs, rsk = rsb.next()
            S.op("dve", lambda: nc.vector.reciprocal(out=rs[:], in_=sm[:]), reads=[smk], writes=[rsk])
            S.op("dve", lambda: nc.vector.tensor_tensor(out=ogrp[:, oi, :], in0=ot[:], in1=rs[:], op=ALU.mult),
                 reads=[otk, rsk], writes=[("ogrp", oi)])
            sq, sqk = sqb.next()
            S.op("act", lambda: nc.scalar.activation(out=sq[:], in_=ogrp[:, oi, :], func=AF.Square),
                 reads=[("ogrp", oi)], writes=[sqk])
            S.op("pe", lambda: nc.tensor.matmul(gss[0][:], lhsT=k.ones[:], rhs=sq[:], start=first, stop=last),
                 reads=[sqk, ("ones",)], writes=[gss[1]], inc=True, pe_acc=True)

        def group_end(nh, chunk0, t, nfeat):
            emit_rstd(k, gss[0][:], gss[1], float(nfeat), rstdg[:], ("rstdg",), tmpf.next())
            for hi in range(nh):
                c = chunk0 + hi
                S.op("dve", lambda hi=hi, c=c: nc.vector.scalar_tensor_tensor(
                    out=mixedT[:, c, t * TW:(t + 1) * TW], in0=ogrp[:, hi, :], scalar=g[:, GOUT + c:GOUT + c + 1],
                    in1=rstdg[:], op0=ALU.mult, op1=ALU.mult),
                    reads=[("ogrp", hi), ("rstdg",), ("gains",)], writes=[("mx", c, t)])

        allch = [(kc, 0) for kc in range(32)]
        for t in range(NT):
            qb, qk = load_Q(0, t)
            for h in range(4):
                kb, kk = load_K(h * 128)
                vb, vkeys = load_V(h)
                hp = h % 2

                def qk_fn(i, st, stk, kb=kb, kk=kk, h=h, hp=hp):
                    kc = allch[i][0]
                    S.op("pe", lambda: nc.tensor.matmul(st[:], lhsT=kb[:, kc * 128:(kc + 1) * 128], rhs=qb[:, h, :], start=True, stop=False),
                         reads=[kk, qk], writes=[stk], inc=False, pe_acc=True)
                    S.op("pe", lambda: nc.tensor.matmul(st[:], lhsT=Kpe[hp * 64:(hp + 1) * 64, kc * 128:(kc + 1) * 128],
                                                        rhs=qb[hp * 64:(hp + 1) * 64, 4 + h // 2, :], start=False, stop=True),
                         reads=[("Kpe",), qk], writes=[stk], inc=True, pe_acc=True)
                head(qk_fn, None, allch, vb, vkeys, SCALE_A, h, h == 0, h == 3, False)
            group_end(4, 0, t, 512)
            qb, qk = load_Q(768, t)
            for kv in range(2):
                kb, kk = load_K(640 + kv * 128)
                vb, vkeys = load_V(4 + kv)
                for gi in range(3):
                    h = kv * 3 + gi

                    def qk_fn(i, st, stk, kb=kb, kk=kk, h=h):
                        kc = allch[i][0]
                        S.op("pe", lambda: nc.tensor.matmul(st[:], lhsT=kb[:, kc * 128:(kc + 1) * 128], rhs=qb[:, h, :], start=True, stop=True),
                             reads=[kk, qk], writes=[stk], inc=True, pe_acc=True)
                    head(qk_fn, None, allch, vb, vkeys, SCALE_H, h, h == 0, h == 5, False)
            group_end(6, 4, t, 768)
            qb, qk = load_Q(1536, t)
            chs = c_chunks(t)
            for h in range(6):
                kb, kk = load_K(896 + h * 128)
                vb, vkeys = load_V(6 + h)

                def qk_fn(i, st, stk, kb=kb, kk=kk, h=h):
                    kc = chs[i][0]
                    S.op("pe", lambda: nc.tensor.matmul(st[:], lhsT=kb[:, kc * 128:(kc + 1) * 128], rhs=qb[:, h, :], start=True, stop=True),
                         reads=[kk, qk], writes=[stk], inc=True, pe_acc=True)
                head(qk_fn, None, chs, vb, vkeys, SCALE_H, h, h == 0, h == 5, True)
            group_end(6, 10, t, 768)


def emit_outproj(k, d_xT, d_wout, d_x1T, mixedT, rstd2, wslot, ws_i, lay=""):
    nc, S = k.nc, k.S
    S.barrier()
    es = contextlib.ExitStack()
    with es:
        ps = [es.enter_context(nc.psum_tensor(lay + "dps%d" % i, [128, 512], F32)) for i in range(7)]
        xo = _mk_rot(k, es, lay + "xo", 3, [128, TW], F32)
        x1b = _mk_rot(k, es, lay + "x1b", 3, [128, TW], F32)
        sqb = _mk_rot(k, es, lay + "dsq", 2, [128, TW], BF16)
        tmpf = _mk_rot(k, es, lay + "dtmp", 2, [128, TW], F32)
        acc = Rot([(ps[i], ("dps", i)) for i in range(3)])
        NWS = len(wslot)
        for cb in range(16):
            i = ws_i[0] % NWS
            ws_i[0] += 1
            S.dma("pool", lambda i=i, cb=cb: nc.gpsimd.dma_start(out=wslot[i][:].rearrange("p a b -> p (a b)"), in_=d_wout[cb]),
                  writes=[("wslot", i)])
            wt, wk = wslot[i], ("wslot", i)
            for t in range(NT):
                a, ak = acc.next()
                for kc in range(16):
                    S.op("pe", lambda kc=kc: nc.tensor.matmul(a[:], lhsT=wt[:, kc, :], rhs=mixedT[:, kc, t * TW:(t + 1) * TW],
                                                              start=(kc == 0), stop=(kc == 15)),
                         reads=[wk, ("mx", kc, t)], writes=[ak], inc=(kc == 15), pe_acc=True)
                xb, xk = xo.next()
                S.dma("sp", lambda: nc.sync.dma_start(out=xb[:], in_=d_xT[cb * 128:(cb + 1) * 128, t * TW:(t + 1) * TW]), writes=[xk])
                x1, x1k = x1b.next()
                S.op("dve", lambda: nc.vector.tensor_tensor(out=x1[:], in0=a[:], in1=xb[:], op=ALU.add),
                     reads=[ak, xk], writes=[x1k])
                S.dma("sp", lambda: nc.sync.dma_start(out=d_x1T[cb * 128:(cb + 1) * 128, t * TW:(t + 1) * TW], in_=x1[:]),
                      reads=[x1k], writes=[("x1T", cb, t)])
                sq, sqk = sqb.next()
                S.op("act", lambda: nc.scalar.activation(out=sq[:], in_=x1[:], func=AF.Square), reads=[x1k], writes=[sqk])
                S.op("pe", lambda: nc.tensor.matmul(ps[3 + t][:], lhsT=k.ones[:], rhs=sq[:], start=(cb == 0), stop=(cb == 15)),
                     reads=[sqk, ("ones",)], writes=[("dps", 3 + t)], inc=True, pe_acc=True)
        for t in range(NT):
            emit_rstd(k, ps[3 + t][:], ("dps", 3 + t), float(D), rstd2[:, t * TW:(t + 1) * TW], ("rstd2", t), tmpf.next())


def emit_ffn(k, d_x1T, d_ff1, d_ff2, d_out, rstd2, wslot, ws_i, final_norm, lay=""):
    nc, S = k.nc, k.S
    S.barrier()
    lay = lay + "sf_"
    es = contextlib.ExitStack()
    with es:
        def sb(name, shape, dt):
            return es.enter_context(nc.sbuf_tensor(lay + name, shape, dt))
        HT = 1024
        ps = [es.enter_context(nc.psum_tensor(lay + "fps%d" % i, [128, 512], F32)) for i in range(8)]
        C = sb("C", [128, 16, HT], F32)
        h2T = sb("h2T", [128, 16, HT], BF16)
        ug = [sb("ug%d" % i, [128, 8, HT], BF16) for i in range(2)]
        tmpf = _mk_rot(k, es, lay + "ftmp", 3, [128, TW], F32)
        sqb = _mk_rot(k, es, lay + "fsq", 2, [128, TW], BF16)
        acc = Rot([(ps[i], ("fps", i)) for i in range(6)])
        ssqb = Rot([(ps[6], ("fps", 6)), (ps[7], ("fps", 7))])
        g = k.gains
        NWS = len(wslot)

        def load_w(src, nslots):
            i = ws_i[0] % NWS
            ws_i[0] += 1
            S.dma("pool", lambda: nc.gpsimd.dma_start(out=wslot[i][:, 0:nslots, :].rearrange("p a b -> p (a b)"), in_=src),
                  writes=[("wslot", i)])
            return wslot[i], ("wslot", i)

        for hf in range(2):
            def gsl(tl):
                return slice(hf * HT + tl * TW, hf * HT + (tl + 1) * TW)

            def lsl(tl):
                return slice(tl * TW, (tl + 1) * TW)
            for c4 in range(4):
                S.dma("sp", lambda c4=c4: nc.sync.dma_start(
                    out=C[:, c4 * 4:(c4 + 1) * 4, :],
                    in_=d_x1T[c4 * 512:(c4 + 1) * 512, hf * HT:(hf + 1) * HT].rearrange("(a p) n -> p a n", p=128)),
                    reads=[("x1T", c4 * 4 + a, hf * 2 + tl) for a in range(4) for tl in range(2)],
                    writes=[("C", c4 * 4 + a, tl) for a in range(4) for tl in range(2)])
            for c in range(16):
                for tl in range(2):
                    eng = "pool" if (c + tl) % 2 == 0 else "dve"
                    e_ = nc.gpsimd if eng == "pool" else nc.vector
                    S.op(eng, lambda c=c, tl=tl, e_=e_: e_.tensor_scalar(out=h2T[:, c, lsl(tl)], in0=C[:, c, lsl(tl)],
                                                                          scalar1=g[:, G2 + c:G2 + c + 1], scalar2=None, op0=ALU.mult),
                         reads=[("C", c, tl), ("gains",)], writes=[("h2T", c, tl)])
            for gi in range(8):
                u = ug[gi % 2]
                uk = "ug%d" % (gi % 2)
                for j in range(8):
                    wt, wk = load_w(d_ff1[gi * 8 + j], 16)
                    for tl in range(2):
                        a, ak = acc.next()
                        for kc in range(16):
                            S.op("pe", lambda kc=kc: nc.tensor.matmul(a[:], lhsT=wt[:, kc, :], rhs=h2T[:, kc, lsl(tl)],
                                                                      start=(kc == 0), stop=(kc == 15)),
                                 reads=[wk, ("h2T", kc, tl)], writes=[ak], inc=(kc == 15), pe_acc=True)
                        tm, tmk = tmpf.next()
                        S.op("dve", lambda: nc.vector.scalar_tensor_tensor(out=tm[:], in0=a[:], scalar=0.0, in1=rstd2[:, gsl(tl)],
                                                                          op0=ALU.max, op1=ALU.mult),
                             reads=[ak, ("rstd2", hf * 2 + tl)], writes=[tmk])
                        S.op("act", lambda: nc.scalar.activation(out=u[:, j, lsl(tl)], in_=tm[:], func=AF.Square),
                             reads=[tmk], writes=[(uk, j, tl)])
                for cb in range(16):
                    wt, wk = load_w(d_ff2[gi, cb], 8)
                    for tl in range(2):
                        a, ak = acc.next()
                        for j in range(8):
                            S.op("pe", lambda j=j: nc.tensor.matmul(a[:], lhsT=wt[:, j, :], rhs=u[:, j, lsl(tl)],
                                                                    start=(j == 0), stop=(j == 7)),
                                 reads=[wk, (uk, j, tl)], writes=[ak], inc=(j == 7), pe_acc=True)
                        S.op("dve", lambda: nc.vector.tensor_tensor(out=C[:, cb, lsl(tl)], in0=a[:], in1=C[:, cb, lsl(tl)], op=ALU.add),
                             reads=[ak, ("C", cb, tl)], writes=[("C", cb, tl)])
            if not final_norm:
                for c4 in range(4):
                    S.dma("sp", lambda c4=c4: nc.sync.dma_start(
                        out=d_out[c4 * 512:(c4 + 1) * 512, hf * HT:(hf + 1) * HT].rearrange("(a p) n -> p a n", p=128),
                        in_=C[:, c4 * 4:(c4 + 1) * 4, :]),
                        reads=[("C", c4 * 4 + a, tl) for a in range(4) for tl in range(2)],
                        writes=[("xout", c4, hf)])
            else:
                for tl in range(2):
                    sb_, sbk = ssqb.next()
                    for c in range(16):
                        sq, sqk = sqb.next()
                        S.op("act", lambda c=c: nc.scalar.activation(out=sq[:], in_=C[:, c, lsl(tl)], func=AF.Square),
                             reads=[("C", c, tl)], writes=[sqk])
                        S.op("pe", lambda c=c: nc.tensor.matmul(sb_[:], lhsT=k.ones[:], rhs=sq[:], start=(c == 0), stop=(c == 15)),
                             reads=[sqk, ("ones",)], writes=[sbk], inc=True, pe_acc=True)
                    rf, rfk = tmpf.next()
                    emit_rstd(k, sb_[:], sbk, float(D), rf[:], rfk, tmpf.next())
                    for c in range(16):
                        S.op("dve", lambda c=c: nc.vector.scalar_tensor_tensor(
                            out=C[:, c, lsl(tl)], in0=C[:, c, lsl(tl)], scalar=g[:, GF + c:GF + c + 1], in1=rf[:],
                            op0=ALU.mult, op1=ALU.mult),
                            reads=[("C", c, tl), rfk, ("gains",)], writes=[("C", c, tl)])
                for c4 in range(4):
                    S.dma("sp", lambda c4=c4: nc.sync.dma_start(
                        out=d_out[c4 * 512:(c4 + 1) * 512, hf * HT:(hf + 1) * HT].rearrange("(a p) n -> p a n", p=128),
                        in_=C[:, c4 * 4:(c4 + 1) * 4, :]),
                        reads=[("C", c4 * 4 + a, tl) for a in range(4) for tl in range(2)],
                        writes=[("xout", c4, hf)])


def c_masks():
    m = np.zeros((28, 128, 512), np.float32)
    kk = np.arange(128)[:, None]
    qq = np.arange(512)[None, :]

    def mult(d):
        ad = np.abs(d)
        return (ad <= 64).astype(np.float32) + ((ad <= 256) & (d % 4 == 0)) + ((ad <= 1024) & (d % 16 == 0))
    for di in range(20):
        delta = di * 128 - 1024
        m[di] = mult(qq - kk - delta)
    for si in range(8):
        s_ = 20 + si
        m[20 + si] = mult(qq + kk + 128 * s_ - 4095)
    return m.astype(ml_dtypes.bfloat16)


def prep_layer_weights2(inp, l):
    wout_u = _units_kmajor(inp["w_out"][l], 16)
    ff1_u = _units_kmajor(inp["w_ff1"][l], 64)
    w2 = inp["w_ff2"][l]
    ff2_u = np.ascontiguousarray(w2.reshape(8, 8, 128, 16, 128).transpose(0, 3, 2, 1, 4)).reshape(8, 16, 128, 1024)
    return dict(wout=wout_u, ff1=ff1_u, ff2=ff2_u)


def build_L2(half, final_norm):
    nc = bass.Bass("TRN2", target_bir_lowering=False)
    k = K()
    k.nc = nc
    d_xT = nc.dram_tensor("xT", [D, TOK], F32, kind="ExternalInput").ap()
    d_QT = nc.dram_tensor("QT", [QT_ROWS, TOK], BF16, kind="ExternalInput").ap()
    d_KTf = nc.dram_tensor("KTf", [2, KT_ROWS, TOK], BF16, kind="ExternalInput").ap()
    d_Vf = nc.dram_tensor("Vf", [2, 12, 128, 16, 128], BF16, kind="ExternalInput").ap()
    d_masks = nc.dram_tensor("masks", [28, 128, TW], BF16, kind="ExternalInput").ap()
    d_wout = nc.dram_tensor("wout", [16, 128, 2048], F32, kind="ExternalInput").ap()
    d_ff1 = nc.dram_tensor("ff1", [64, 128, 2048], F32, kind="ExternalInput").ap()
    d_ff2 = nc.dram_tensor("ff2", [8, 16, 128, 1024], F32, kind="ExternalInput").ap()
    k.d_gains = nc.dram_tensor("gains", [128, NG], F32, kind="ExternalInput").ap()
    k.d_perm = nc.dram_tensor("perm", [3, 128, 128], F32, kind="ExternalInput").ap()
    d_x1T = nc.dram_tensor("x1T", [D, TOK], F32, kind="Internal").ap()
    d_out = nc.dram_tensor("xout", [D, TOK], F32, kind="ExternalOutput").ap()
    if DEBUG:
        k.dbg = nc.dram_tensor("dbg", [16, 128, TOK], BF16, kind="ExternalOutput").ap()
    es = contextlib.ExitStack()
    with es:
        k.es = es
        k.S = Sched(nc, es)
        k.epsb = es.enter_context(nc.sbuf_tensor("s_epsb", [128, 1], F32))
        k.S.op("dve", lambda: nc.vector.memset(k.epsb[:], EPS), writes=[("epsb",)])
        setup_common(k)
        rstd2 = es.enter_context(nc.sbuf_tensor("s_rstd2", [128, TOK], F32))
        wslot = [es.enter_context(nc.sbuf_tensor("s_wslot%d" % i, [128, 16, 128], BF16)) for i in range(4)]
        ws_i = [0]
        es2 = contextlib.ExitStack()
        with es2:
            mixedT = es2.enter_context(nc.sbuf_tensor("s_mixedT", [128, 16, TOK], BF16))
            emit_attn(k, half, d_QT, d_KTf, d_Vf, d_masks, mixedT)
            if getattr(k, "dbg", None) is not None:
                for c in range(16):
                    k.S.dma("sp", lambda c=c: nc.sync.dma_start(out=k.dbg[c], in_=mixedT[:, c, :]),
                            reads=[("mx", c, t) for t in range(NT)], writes=[("dbg", c)])
            emit_outproj(k, d_xT, d_wout, d_x1T, mixedT, rstd2, wslot, ws_i)
        emit_ffn(k, d_x1T, d_ff1, d_ff2, d_out, rstd2, wslot, ws_i, final_norm)
        k.S.finish("sp")
        print("L2 stats", k.S.stats())
    return nc


_PROGS = {}


def _prog(name):
    if name not in _PROGS:
        if name == "L1":
            _PROGS[name] = build_L1()
        elif name == "L2":
            _PROGS[name] = build_L2(0, False)
        else:
            _PROGS[name] = build_L2(0, True)
    return _PROGS[name]


def kernel(**inp):
    inp = {k_: np.asarray(v) for k_, v in inp.items()}
    x = inp["x"].astype(np.float32, copy=False)
    ncores = 8
    cores = list(range(ncores))
    xs = []
    for c in cores:
        b, half = c // 2, c % 2
        xb = x[b, :TOK] if half == 0 else x[b, ::-1][:TOK]
        xs.append(np.ascontiguousarray(xb.T))
    pm = perm_mats()
    masks = c_masks()
    tabs = [rope_tables(0), rope_tables(1)]
    for l in range(DEPTH):
        lw = prep_layer_weights(inp, l)
        in1 = [dict(xT=xs[c], win=lw["win"], wuq=lw["wuq"], wukv=lw["wukv"], gains=lw["gains"],
                    tabs=tabs[c % 2], perm=pm) for c in cores]
        r1 = run_bass_kernel_spmd(_prog("L1"), in1, core_ids=cores).results
        r1 = [{k_: np.asarray(v) for k_, v in r.items()} for r in r1]
        del in1
        lw2 = prep_layer_weights2(inp, l)
        in2 = []
        for c in cores:
            p = c ^ 1
            in2.append(dict(xT=xs[c], QT=r1[c]["QT"], KTf=np.stack([r1[c]["KT"], r1[p]["KT"]]),
                            Vf=np.stack([r1[c]["V"], r1[p]["V"]]), masks=masks, wout=lw2["wout"],
                            ff1=lw2["ff1"], ff2=lw2["ff2"], gains=lw["gains"], perm=pm))
        r2 = run_bass_kernel_spmd(_prog("L2" if l < DEPTH - 1 else "L2f"), in2, core_ids=cores).results
        xs = [np.asarray(r["xout"]) for r in r2]
        del in2, lw, lw2
    out = np.empty((4, SEQ, D), np.float32)
    for c in cores:
        b, half = c // 2, c % 2
        if half == 0:
            out[b, :TOK] = xs[c].T
        else:
            out[b, TOK:] = xs[c].T[::-1]
    return out
```
